# Optimizing a Trainium2 kernel written in Bass

```python
import math
import jax, jax.numpy as jnp
from jax import lax
import numpy as np

D_MODEL = 1024
BATCH = 4
SEQ = 8192
DEPTH = 4

N_HEADS = 4
BRANCH_W = D_MODEL // 2
N_BRANCH = 3
GLA_DK = D_MODEL // 16
GLA_DV = BRANCH_W // N_HEADS
GLA_RANK = 16
GLA_TAU = 16.0
GLA_CHUNK = 64
RET_DK = D_MODEL // 16
RET_DV = BRANCH_W // N_HEADS
RET_CHUNK = 128
RET_ROPE_BASE = 10000.0
DIL_HD = BRANCH_W // N_HEADS
DIL_GROUPS = ((128, 1), (512, 4), (2048, 16))
N_DIL = len(DIL_GROUPS)
ROPE_THETA = 500000.0
ROPE_DIMS = DIL_HD // 4
D_FF = -(-8 * D_MODEL // (3 * 256)) * 256
ALPHA = (2 * DEPTH) ** 0.25
BETA = (8 * DEPTH) ** -0.25

IN_WIDTHS = (N_HEADS * GLA_DK, N_HEADS * GLA_DK, BRANCH_W, BRANCH_W, GLA_RANK,
             N_HEADS * RET_DK, N_HEADS * RET_DK, BRANCH_W, BRANCH_W,
             N_DIL * BRANCH_W, N_DIL * BRANCH_W, N_DIL * BRANCH_W,
             N_BRANCH * D_MODEL)
D_IN = sum(IN_WIDTHS)
IN_SPLITS = tuple(int(s) for s in np.cumsum(IN_WIDTHS)[:-1])

kernel_name = 'hybrid_gated_gla_retention_dilated_attn'

F32 = jnp.float32


def _layer_norm(x, g, b, eps=1e-5):
    xf = x.astype(F32)
    mu = jnp.mean(xf, axis=-1, keepdims=True)
    var = jnp.mean(jnp.square(xf - mu), axis=-1, keepdims=True)
    return ((xf - mu) * lax.rsqrt(var + eps) * g.astype(F32) + b.astype(F32)).astype(x.dtype)


def _head_rms_norm(o, g, eps=1e-6):
    B, S, H, D = o.shape
    y = o * lax.rsqrt(jnp.mean(jnp.square(o), axis=-1, keepdims=True) + eps)
    return (y * g.astype(F32).reshape(H, D)).reshape(B, S, H * D)


def _head_group_norm(o, g, eps=1e-5):
    B, S, H, D = o.shape
    mu = jnp.mean(o, axis=-1, keepdims=True)
    var = jnp.mean(jnp.square(o - mu), axis=-1, keepdims=True)
    y = (o - mu) * lax.rsqrt(var + eps)
    return (y * g.astype(F32).reshape(H, D)).reshape(B, S, H * D)


def _rotary(x, pos, n_rot, base):
    half = n_rot // 2
    inv = base ** (-jnp.arange(half, dtype=F32) * 2.0 / n_rot)
    ang = pos[:, None] * inv[None, :]
    cos = jnp.cos(ang)[None, :, None, :]
    sin = jnp.sin(ang)[None, :, None, :]
    xf = x.astype(F32)
    x1 = xf[..., :half]
    x2 = xf[..., half:n_rot]
    return jnp.concatenate([x1 * cos - x2 * sin, x2 * cos + x1 * sin, xf[..., n_rot:]], axis=-1)


def _gla_chunked(q, k, v, log_a):
    B, S, H, DK = q.shape
    DV = v.shape[-1]
    C = GLA_CHUNK
    N = S // C
    q = q.astype(F32).reshape(B, N, C, H, DK) * (DK ** -0.5)
    k = k.astype(F32).reshape(B, N, C, H, DK)
    v = v.astype(F32).reshape(B, N, C, H, DV)
    b = jnp.cumsum(log_a.astype(F32).reshape(B, N, C, H, DK), axis=2)
    b_last = b[:, :, -1:]
    q_s = q * jnp.exp(b)
    k_s = k * jnp.exp(-b)
    causal = jnp.tril(jnp.ones((C, C), dtype=bool))
    att = jnp.where(causal, jnp.einsum('bnihd,bnjhd->bnhij', q_s, k_s), 0.0)
    o_intra = jnp.einsum('bnhij,bnjhe->bnihe', att, v)
    kv = jnp.einsum('bnjhd,bnjhe->bnhde', k * jnp.exp(b_last - b), v)
    decay = jnp.exp(b_last[:, :, 0])

    def step(state, inp):
        dec, kv_n = inp
        return dec[..., None] * state + kv_n, state

    s0 = jnp.zeros((B, H, DK, DV), F32)
    _, s_prev = lax.scan(step, s0, (jnp.moveaxis(decay, 1, 0), jnp.moveaxis(kv, 1, 0)))
    s_prev = jnp.moveaxis(s_prev, 0, 1)
    o_inter = jnp.einsum('bnihd,bnhde->bnihe', q_s, s_prev)
    return (o_intra + o_inter).reshape(B, S, H, DV)


def _retention_chunked(q, k, v):
    B, S, H, DK = q.shape
    DV = v.shape[-1]
    C = RET_CHUNK
    N = S // C
    log_g = jnp.log1p(-jnp.exp2(-5.0 - jnp.arange(H, dtype=F32)))
    q = q.astype(F32).reshape(B, N, C, H, DK)
    k = k.astype(F32).reshape(B, N, C, H, DK) * (DK ** -0.5)
    v = v.astype(F32).reshape(B, N, C, H, DV)
    idx = jnp.arange(C, dtype=F32)
    diff = idx[:, None] - idx[None, :]
    dmat = jnp.where(diff >= 0, jnp.exp(jnp.maximum(diff, 0.0)[None] * log_g[:, None, None]), 0.0)
    att = jnp.einsum('bnihd,bnjhd->bnhij', q, k) * dmat
    o_intra = jnp.einsum('bnhij,bnjhe->bnihe', att, v)
    q_dec = q * jnp.exp((idx + 1.0)[:, None] * log_g[None, :])[:, :, None]
    k_dec = k * jnp.exp((C - 1.0 - idx)[:, None] * log_g[None, :])[:, :, None]
    kv = jnp.einsum('bnjhd,bnjhe->bnhde', k_dec, v)
    chunk_decay = jnp.exp(C * log_g)[:, None, None]

    def step(state, kv_n):
        return chunk_decay * state + kv_n, state

    s0 = jnp.zeros((B, H, DK, DV), F32)
    _, s_prev = lax.scan(step, s0, jnp.moveaxis(kv, 1, 0))
    s_prev = jnp.moveaxis(s_prev, 0, 1)
    o_inter = jnp.einsum('bnihd,bnhde->bnihe', q_dec, s_prev)
    return (o_intra + o_inter).reshape(B, S, H, DV)


def _dilated_window_attention(q, k, v, window, dilation):
    B, S, H, D = q.shape
    span = window // dilation
    n = S // dilation
    nb = -(-n // span)
    n_pad = nb * span

    def strided(t):
        t = t.astype(F32).reshape(B, n, dilation, H, D)
        return jnp.pad(t, ((0, 0), (0, n_pad - n), (0, 0), (0, 0), (0, 0)))

    def band(t):
        tp = jnp.pad(t, ((0, 0), (span, 0), (0, 0), (0, 0), (0, 0))).reshape(B, nb + 1, span, dilation, H, D)
        return jnp.concatenate([tp[:, :-1], tp[:, 1:]], axis=2)

    qb = strided(q).reshape(B, nb, span, dilation, H, D)
    kb = band(strided(k))
    vb = band(strided(v))
    s = jnp.einsum('bnichd,bnjchd->bnchij', qb, kb) * (D ** -0.5)
    i = jnp.arange(span)[:, None]
    j = jnp.arange(2 * span)[None, :]
    dist = i + span - j
    key_idx = jnp.arange(nb)[:, None, None] * span + j[None] - span
    mask = (dist >= 0)[None] & (dist <= span)[None] & (key_idx >= 0)
    s = jnp.where(mask[None, :, None, None], s, -jnp.inf)
    m = jnp.max(s, axis=-1)
    p = jnp.exp(s - m[..., None])
    den = jnp.sum(p, axis=-1)
    o = jnp.einsum('bnchij,bnjchd->bnichd', p, vb) / jnp.moveaxis(den, -1, 2)[..., None]
    lse = jnp.moveaxis(m + jnp.log(den), -1, 2)
    o = o.reshape(B, n_pad, dilation, H, D)[:, :n].reshape(B, S, H, D)
    lse = lse.reshape(B, n_pad, dilation, H)[:, :n].reshape(B, S, H)
    return o, lse


def _mixer(x, w_in, w_gla_a2, b_gla_a, gla_norm_g, ret_norm_g, w_branch, b_gate, w_out):
    B, S, _ = x.shape
    pos = jnp.arange(S, dtype=F32)
    proj = x @ w_in
    (gq, gk, gv, gr, ga, rq, rk, rv, rg, dq, dk, dv, gate) = jnp.split(proj, IN_SPLITS, axis=-1)

    log_a = jax.nn.log_sigmoid((ga @ w_gla_a2 + b_gla_a).astype(F32)) / GLA_TAU
    o_a = _gla_chunked(gq.reshape(B, S, N_HEADS, GLA_DK), gk.reshape(B, S, N_HEADS, GLA_DK),
                       gv.reshape(B, S, N_HEADS, GLA_DV), log_a.reshape(B, S, N_HEADS, GLA_DK))
    o_a = _head_rms_norm(o_a, gla_norm_g) * jax.nn.silu(gr.astype(F32))

    q_r = _rotary(rq.reshape(B, S, N_HEADS, RET_DK), pos, RET_DK, RET_ROPE_BASE)
    k_r = _rotary(rk.reshape(B, S, N_HEADS, RET_DK), pos, RET_DK, RET_ROPE_BASE)
    o_b = _retention_chunked(q_r, k_r, rv.reshape(B, S, N_HEADS, RET_DV))
    o_b = _head_group_norm(o_b, ret_norm_g) * jax.nn.silu(rg.astype(F32))

    dq = dq.reshape(B, S, N_DIL, N_HEADS, DIL_HD)
    dk = dk.reshape(B, S, N_DIL, N_HEADS, DIL_HD)
    dv = dv.reshape(B, S, N_DIL, N_HEADS, DIL_HD)
    outs = []
    lses = []
    for g, (window, dilation) in enumerate(DIL_GROUPS):
        qg = _rotary(dq[:, :, g], pos, ROPE_DIMS, ROPE_THETA)
        kg = _rotary(dk[:, :, g], pos, ROPE_DIMS, ROPE_THETA)
        o_g, lse_g = _dilated_window_attention(qg, kg, dv[:, :, g], window, dilation)
        outs.append(o_g)
        lses.append(lse_g)
    wts = jax.nn.softmax(jnp.stack(lses, axis=0), axis=0)
    o_c = jnp.sum(wts[..., None] * jnp.stack(outs, axis=0), axis=0).reshape(B, S, BRANCH_W)

    y_a = o_a.astype(x.dtype) @ w_branch[0]
    y_b = o_b.astype(x.dtype) @ w_branch[1]
    y_c = o_c.astype(x.dtype) @ w_branch[2]
    gates = jax.nn.sigmoid((gate + b_gate).astype(F32)).reshape(B, S, N_BRANCH, D_MODEL).astype(x.dtype)
    merged = gates[:, :, 0] * y_a + gates[:, :, 1] * y_b + gates[:, :, 2] * y_c
    return merged @ w_out


def _swiglu(x, w_gate, w_up, w_down):
    return (jax.nn.silu(x @ w_gate) * (x @ w_up)) @ w_down


def setup_inputs(seed: int = 0) -> dict:
    key = jax.random.key(seed)
    ks = jax.random.split(key, 16)
    nrm = jax.random.normal
    return {
        'x': nrm(ks[0], (BATCH, SEQ, D_MODEL), F32),
        'w_in': nrm(ks[1], (DEPTH, D_MODEL, D_IN), F32) * D_MODEL ** -0.5,
        'w_gla_a2': nrm(ks[2], (DEPTH, GLA_RANK, N_HEADS * GLA_DK), F32) * GLA_RANK ** -0.5,
        'b_gla_a': 0.1 * nrm(ks[3], (DEPTH, N_HEADS * GLA_DK), F32),
        'gla_norm_g': 1.0 + 0.02 * nrm(ks[4], (DEPTH, BRANCH_W), F32),
        'ret_norm_g': 1.0 + 0.02 * nrm(ks[5], (DEPTH, BRANCH_W), F32),
        'w_branch': nrm(ks[6], (DEPTH, N_BRANCH, BRANCH_W, D_MODEL), F32) * BRANCH_W ** -0.5,
        'b_gate': 0.02 * nrm(ks[7], (DEPTH, N_BRANCH * D_MODEL), F32),
        'w_out': nrm(ks[8], (DEPTH, D_MODEL, D_MODEL), F32) * (D_MODEL ** -0.5 * BETA),
        'ln1_g': 1.0 + 0.02 * nrm(ks[9], (DEPTH, D_MODEL), F32),
        'ln1_b': 0.02 * nrm(ks[10], (DEPTH, D_MODEL), F32),
        'w_ffn_gate': nrm(ks[11], (DEPTH, D_MODEL, D_FF), F32) * D_MODEL ** -0.5,
        'w_ffn_up': nrm(ks[12], (DEPTH, D_MODEL, D_FF), F32) * D_MODEL ** -0.5,
        'w_ffn_down': nrm(ks[13], (DEPTH, D_FF, D_MODEL), F32) * (D_FF ** -0.5 * BETA),
        'ln2_g': 1.0 + 0.02 * nrm(ks[14], (DEPTH, D_MODEL), F32),
        'ln2_b': 0.02 * nrm(ks[15], (DEPTH, D_MODEL), F32),
    }


def reference(x, w_in, w_gla_a2, b_gla_a, gla_norm_g, ret_norm_g, w_branch, b_gate, w_out,
              ln1_g, ln1_b, w_ffn_gate, w_ffn_up, w_ffn_down, ln2_g, ln2_b):
    for l in range(DEPTH):
        h = _mixer(x, w_in[l], w_gla_a2[l], b_gla_a[l], gla_norm_g[l], ret_norm_g[l],
                   w_branch[l], b_gate[l], w_out[l])
        x = _layer_norm(ALPHA * x + h, ln1_g[l], ln1_b[l])
        f = _swiglu(x, w_ffn_gate[l], w_ffn_up[l], w_ffn_down[l])
        x = _layer_norm(ALPHA * x + f, ln2_g[l], ln2_b[l])
    return x
```

```python
import numpy as np
from contextlib import ExitStack
import concourse.bass as bass
import concourse.mybir as mybir
from concourse.bass_utils import run_bass_kernel_spmd

F32 = mybir.dt.float32
BF16 = mybir.dt.bfloat16
AF = mybir.ActivationFunctionType
ALU = mybir.AluOpType

D = 1024
DEPTH = 4
DFF = 2816
ALPHA = (2 * DEPTH) ** 0.25
TT = 512
NWS = 4

C_ID, C_PR, C_PD, C_CA, C_DM, C_GQ, C_GK, C_CD, C_RS = 0, 128, 256, 384, 512, 1024, 1280, 1536, 1538
C_M3 = 1538 + 512
NCST = C_M3 + 128
P_BG, P_GG, P_RG, P_L1G, P_L1B, P_L2G, P_L2B = 0, 24, 28, 32, 40, 48, 56


class Op:
    __slots__ = ("eng", "fn", "deps", "dma", "sig", "prev", "idx", "need")

    def __init__(self, eng, fn, dma):
        self.eng, self.fn, self.dma = eng, fn, dma
        self.deps = ()
        self.sig = None
        self.prev = None
        self.need = False


class Sched:
    def __init__(self):
        self.ops = []
        self.lastw = {}
        self.rd = {}

    def add(self, eng, fn, R=(), W=(), dma=False):
        op = Op(eng, fn, dma)
        deps = set()
        for r in R:
            w = self.lastw.get(r)
            if w is not None:
                deps.add(w)
        for k in W:
            w = self.lastw.get(k)
            if w is not None:
                deps.add(w)
            rr = self.rd.get(k)
            if rr:
                for o in rr[0].values():
                    deps.add(o)
                for o in rr[1]:
                    deps.add(o)
        op.deps = tuple(deps)
        for k in W:
            self.lastw[k] = op
            self.rd[k] = ({}, [])
        for r in R:
            if r in W:
                continue
            rr = self.rd.setdefault(r, ({}, []))
            if dma:
                rr[1].append(op)
            else:
                rr[0][eng] = op
        op.idx = len(self.ops)
        self.ops.append(op)
        return op

    def emit(self, nc, es, block_engines):
        engs = ["pe", "act", "dve", "pool", "sp"]
        NDSQ = {"sp": 12, "pool": 2, "act": 2}
        esem = {e: es.enter_context(nc.semaphore("s_" + e)) for e in engs}
        dsem = {e: [es.enter_context(nc.semaphore("d_%s%d" % (e, i))) for i in range(NDSQ[e])] for e in ("sp", "pool", "act")}
        for op in self.ops:
            for d in op.deps:
                if d.dma:
                    continue
                if d.eng != op.eng or d.eng != "pe":
                    d.need = True
        cnt = {e: 0 for e in engs}
        nd = {e: 0 for e in dsem}
        lastd = {}
        for op in self.ops:
            if op.dma:
                q = op.eng
                k = nd[q] % NDSQ[q]
                v = 16 * (nd[q] // NDSQ[q] + 1)
                nd[q] += 1
                op.sig = (("d", q, k), v)
                op.prev = lastd.get((q, k))
                lastd[(q, k)] = op
            elif op.need:
                cnt[op.eng] += 1
                op.sig = (("e", op.eng), cnt[op.eng])

        def semof(key):
            return esem[key[1]] if key[0] == "e" else dsem[key[1]][key[2]]

        per = {e: [] for e in engs}
        for op in self.ops:
            per[op.eng].append(op)
        know = {e: {} for e in engs}
        after = {}
        self.nwait = 0

        plan = {e: [] for e in engs}
        for op in self.ops:
            e = op.eng
            kn = know[e]
            wl = []
            waits = list(op.deps)
            if op.prev is not None:
                waits.append(op.prev)
            for d in waits:
                if d.sig is None or (d.eng == "pe" and e == "pe" and not d.dma):
                    continue
                key, v = d.sig
                if kn.get(key, 0) >= v:
                    continue
                wl.append((key, v))
                ka = after.get(d.idx)
                if ka:
                    for k2, v2 in ka.items():
                        if kn.get(k2, 0) < v2:
                            kn[k2] = v2
                kn[key] = v
            if op.sig is not None:
                ka = dict(kn)
                ka[op.sig[0]] = op.sig[1]
                after[op.idx] = ka
            plan[e].append((op, wl))
            self.nwait += len(wl)

        def body(e):
            def f(h):
                for op, wl in plan[e]:
                    for key, v in wl:
                        h.wait_ge(semof(key), v)
                    ins = op.fn(h)
                    if op.sig is not None:
                        ins.then_inc(semof(op.sig[0]), 16 if op.dma else 1)
                if e in dsem:
                    for k in range(NDSQ[e]):
                        o = lastd.get((e, k))
                        if o is not None:
                            h.wait_ge(semof(o.sig[0]), o.sig[1])
            return f

        block = es.enter_context(nc.Block())
        block.tensor(body("pe"))
        block.scalar(body("act"))
        block.vector(body("dve"))
        block.gpsimd(body("pool"))
        block.sync(body("sp"))


def build(NT_CTX, NT_OWN, debug=False):
    nc = bass.Bass("TRN2", target_bir_lowering=False)
    NT = NT_CTX + NT_OWN
    L = NT * TT
    S = Sched()
    dr = {}

    def din(name, shape, dt=F32):
        dr[name] = nc.dram_tensor(name, list(shape), dt, kind="ExternalInput").ap()

    def dscr(name, shape, dt=BF16):
        dr[name] = nc.dram_tensor(name, list(shape), dt, kind="Internal").ap()

    din("xT", [D, L])
    din("rot", [128, 4, L])
    din("maskp", [NT_OWN, 128, 3, TT])
    din("cst", [128, NCST])
    din("prm", [128, 64])
    wshapes = {"win": [D, 7680], "wga": [D, 128], "wa2": [128, 256], "wgb": [1536, 3072],
               "wout": [D, D], "wg": [D, DFF], "wu": [D, DFF], "wd": [DFF, D]}
    for k, shp in wshapes.items():
        din(k, shp)
        dscr(k + "_b", shp)
    dscr("v3scr", [2, TT, 512])
    yT = nc.dram_tensor("yT", [D, NT_OWN * TT], F32, kind="ExternalOutput").ap()
    dbg = {}
    if debug:
        for nm in ("oa", "ob", "oc", "mg"):
            dbg[nm] = nc.dram_tensor("dbg_" + nm, [128, 4 if nm != "mg" else 8, TT], F32, kind="ExternalOutput").ap()
        dbg["x1"] = nc.dram_tensor("dbg_x1", [128, 8, TT], F32, kind="ExternalOutput").ap()

    es = ExitStack()
    sb = {}

    def st(name, shape, dt):
        sb[name] = es.enter_context(nc.sbuf_tensor("sb_" + name, list(shape), dt))
        return sb[name]

    cst = st("cst", [128, NCST], F32)
    cb = st("cb", [128, 6, 128], BF16)
    prm = st("prm", [128, 64], F32)
    wa2 = st("wa2", [128, 256], BF16)
    gaT = st("gaT", [128, TT], BF16)
    wga = st("wga", [128, 8, 128], BF16)
    xTf = st("xTf", [128, 8, TT], F32)
    xTb = st("xTb", [128, 8, TT], BF16)
    ws = [st("w%d" % i, [128, 2048], BF16) for i in range(NWS)]
    rot = st("rot", [128, 2, TT], F32)
    mkp = st("mkp", [128, 3, TT], BF16)
    sc = [st("sc%d" % i, [128, TT], F32)[:, :] for i in range(6)]
    qb = st("qb", [128, 4, TT], BF16)
    kb = st("kb", [128, 4, TT], BF16)
    qm = st("qm", [128, 4, TT], BF16)
    qd = st("qd", [128, 4, TT], BF16)
    V3x = st("V3x", [128, 4, 512], BF16)
    vt = st("vt", [128, 4, TT], BF16)
    att = st("att", [128, 2, TT], BF16)
    ktk = st("ktk", [128, 2, 256], BF16)
    stmp = st("stmp", [128, 2, 128], F32)
    Sg = st("Sg", [128, 2, 128], F32)
    Sgb = st("Sgb", [128, 2, 128], BF16)
    Sr = st("Sr", [128, 2, 128], F32)
    Srb = st("Srb", [128, 2, 128], BF16)
    oT = {k: st(k, [128, 4, TT], BF16) for k in ("oa", "ob", "oc")}
    K1 = st("K1", [128, 4, TT], BF16)
    K1p = st("K1p", [128, 4, 128], BF16)
    V1 = st("V1", [128, 4, TT], BF16)
    V1p = st("V1p", [128, 512], BF16)
    K2 = [st("K2%d" % i, [128, 4, TT], BF16) for i in range(2)]
    V2 = [st("V2%d" % i, [128, 4, 512], BF16) for i in range(2)]
    K3c = st("K3c", [128, 4, TT], BF16)
    K3r = st("K3r", [128, 4, 16, 128], BF16)
    V3r = st("V3r", [128, 16, 512], BF16)
    big = st("big", [128, 22 * TT], BF16)
    pp = st("pp", [128, 2, TT], BF16)
    ee = st("ee", [128, 2, TT], BF16)
    rq = st("rq", [128, 2, TT], BF16)

    B = [es.enter_context(nc.psum_tensor("b%d" % i, [128, TT], F32))[:, :] for i in range(7)]
    BT = es.enter_context(nc.psum_tensor("bt", [128, 1024], BF16))[:, :]

    ident_b, permr_b, permd_b, ones_b, od128_b, od1024_b = (cb[:, i, :] for i in range(6))
    mgT = big[:, 0:8 * TT].rearrange("p (a t) -> p a t", a=8)
    actT = big[:, :].rearrange("p (a t) -> p a t", a=22)
    V3c = big[0:32, 0:16 * 512].rearrange("p (c e) -> p c e", c=16)
    causal = cst[:, C_CA:C_CA + 128]
    causal4 = causal.unsqueeze(1).broadcast_to([128, 4, 128])
    m3c4 = cst[:, C_M3:C_M3 + 128].unsqueeze(1).broadcast_to([128, 4, 128])
    dmaskr = cst[:, C_DM:C_DM + 512]
    reset = cst[:, C_RS:C_RS + 512]

    def bk(i):
        return ("B", i)

    class BankPool:
        def __init__(self):
            self.free = list(range(7))

        def get(self):
            return self.free.pop(0)

        def put(self, b):
            self.free.append(b)

    bp = BankPool()

    def mm(out, lhsT, rhs, start=True, stop=True, R=(), W=()):
        S.add("pe", lambda e: e.matmul(out, lhsT=lhsT, rhs=rhs, start=start, stop=stop, skip_group_check=True), R, W)

    def tr(out, in_, ident, R=(), W=()):
        S.add("pe", lambda e: e.transpose(out, in_, ident), R, W)

    def act(out, in_, func, R=(), W=(), scale=1.0, bias=0.0):
        S.add("act", lambda e: e.activation(out=out, in_=in_, func=func, bias=bias, scale=scale), R, W)

    def tt(out, in0, in1, op, R=(), W=()):
        S.add("dve", lambda e: e.tensor_tensor(out=out, in0=in0, in1=in1, op=op), R, W)

    def ts(out, in0, s1, s2, op0, op1, R=(), W=()):
        if s2 is None:
            S.add("dve", lambda e: e.tensor_scalar(out=out, in0=in0, scalar1=s1, scalar2=None, op0=op0), R, W)
        else:
            S.add("dve", lambda e: e.tensor_scalar(out=out, in0=in0, scalar1=s1, scalar2=s2, op0=op0, op1=op1), R, W)

    def stt(out, in0, scalar, in1, op0, op1, R=(), W=()):
        S.add("dve", lambda e: e.scalar_tensor_tensor(out=out, in0=in0, scalar=scalar, in1=in1, op0=op0, op1=op1), R, W)

    def recip(out, in_, R=(), W=()):
        S.add("dve", lambda e: e.reciprocal(out=out, in_=in_), R, W)

    def vcopy(out, in_, R=(), W=()):
        S.add("dve", lambda e: e.tensor_copy(out=out, in_=in_), R, W)

    def dma(q, out, in_, R=(), W=(), **kw):
        S.add(q, lambda e: e.dma_start(out=out, in_=in_, **kw), R, W, dma=True)

    def memset(eng, ap, val, W=()):
        S.add(eng, lambda e: e.memset(ap, val), (), W)

    wctr = [0]

    def wload(view, kc, ncols, wname):
        s = wctr[0] % NWS
        wctr[0] += 1
        dst = ws[s][:, 0:kc * ncols].rearrange("p (a n) -> p a n", a=kc)
        dma("sp", dst, view, R=[("wb", wname)], W=[("w", s)])
        return s, dst

    dma("sp", cst[:, :], dr["cst"], W=["cst"])
    dma("sp", prm[:, :], dr["prm"], W=["prm"])
    for i, off in enumerate((C_ID, C_PR, C_PD)):
        vcopy(cb[:, i, :], cst[:, off:off + 128], R=["cst"], W=["cb"])
    memset("dve", cb[:, 3, :], 1.0, W=["cb"])
    memset("dve", cb[:, 4, :], 1.0 / 128, W=["cb"])
    memset("dve", cb[:, 5, :], 1.0 / 1024, W=["cb"])
    memset("dve", gaT[:, :], 1.0, W=["gaT"])
    for nm, t in (("Sg", Sg), ("Sr", Sr), ("Sgb", Sgb), ("Srb", Srb)):
        memset("dve", t[:, :, :], 0.0, W=[nm])
    memset("dve", K3r[:, :, :, :], 0.0, W=["K3r"])
    memset("dve", qm[:, :, :], 0.0, W=["qm"])
    memset("dve", qd[:, :, :], 0.0, W=["qd"])
    memset("dve", V3r[:, :, :], 0.0, W=["V3r"])
    memset("dve", K1p[:, :, :], 0.0, W=["K1p"])
    memset("dve", V1p[:, :], 0.0, W=["V1p"])
    for i in range(2):
        memset("dve", K2[i][:, :, :], 0.0, W=[("K2", i)])
        memset("dve", V2[i][:, :, :], 0.0, W=[("V2", i)])

    def conv(name, nsplit, inner):
        src, dst = dr[name], dr[name + "_b"]
        rows = src.shape[0]
        step = rows // nsplit
        for i in range(nsplit):
            a = src[i * step:(i + 1) * step, :].rearrange("r (a b) -> r a b", b=inner)
            b = dst[i * step:(i + 1) * step, :].rearrange("r (a b) -> r a b", b=inner)
            dma("pool", b, a, R=(), W=[("wb", name)])

    conv("wga", 1, 128)
    conv("wa2", 1, 256)
    conv("win", 8, 512)
    conv("wgb", 4, 512)
    conv("wout", 1, 512)
    conv("wg", 4, 256)
    conv("wu", 4, 256)
    conv("wd", 4, 512)
    dma("sp", wa2[:, :], dr["wa2_b"], R=[("wb", "wa2")], W=["wa2"])
    dma("sp", wga[:, :, :], dr["wga_b"].rearrange("(kc p) n -> p kc n", p=128), R=[("wb", "wga")], W=["wga"])

    winv = dr["win_b"].rearrange("(kc p) n -> p kc n", p=128)
    wgbv = dr["wgb_b"].rearrange("(r p) n -> p r n", p=128)
    woutv = dr["wout_b"].rearrange("(kc p) n -> p kc n", p=128)
    wgv = dr["wg_b"].rearrange("(kc p) n -> p kc n", p=128)
    wuv = dr["wu_b"].rearrange("(kc p) n -> p kc n", p=128)
    wdv = dr["wd_b"].rearrange("(r p) n -> p r n", p=128)
    xTv = dr["xT"].rearrange("(kc p) t -> p kc t", p=128)
    yTv = yT.rearrange("(kc p) t -> p kc t", p=128)

    def proj_fm(bank, wslot, wap, coff, rows=128):
        for kc in range(8):
            mm(B[bank][0:rows, :], wap[:, kc, coff:coff + rows], xTb[:, kc, :], kc == 0, kc == 7,
               R=[("w", wslot), "xTb"], W=[bk(bank)])

    def proj_tm(bank, sl, wslot, wap, tok_ap_fn, ncols=256, coff=0):
        for kc in range(8):
            mm(B[bank][:, coff:coff + ncols], tok_ap_fn(kc), wap[:, kc, 0:ncols], kc == 0, kc == 7,
               R=[("w", wslot), "xTb"], W=[bk(bank)])

    def wl_win(c0):
        return wload(winv[:, :, c0:c0 + 256], 8, 256, "win")

    CQ_G, CK_G, CV_G, CR_G = 0, 256, 512, 1024
    CQ_R, CK_R, CV_R, CR_R = 1536, 1792, 2048, 2560
    CV_D = 3072
    CQK_D = 4608

    def rotary(bank, dst, perm_b, R_extra=()):
        i = rotary.i = (getattr(rotary, "i", 0) + 1) % 2
        act(rq[:, i, :], B[bank], AF.Copy, R=[bk(bank)], W=[("rq", i)])
        b2 = bp.get()
        mm(B[b2], perm_b, rq[:, i, :], R=[("rq", i), "cb"], W=[bk(b2)])
        tt(sc[4], rq[:, i, :], rot[:, 0, :], ALU.mult, R=[("rq", i), "rot"], W=[("sc", 4)])
        tt(sc[5], B[b2], rot[:, 1, :], ALU.mult, R=[bk(b2), "rot"], W=[("sc", 5)])
        bp.put(b2)
        return lambda out, Wk: tt(out, sc[4], sc[5], ALU.add, R=[("sc", 4), ("sc", 5)], W=Wk)

    def tile_step(T):
        t0 = T * TT
        own = T >= NT_CTX
        n = T - NT_CTX
        par = T % 2
        need3 = own or T >= NT_CTX - 4
        need12 = own or T == NT_CTX - 1
        last = (T == NT - 1)
        dma("sp", xTf[:, :, :], xTv[:, :, t0:t0 + TT], W=["xTf"] + [("xTf", kc) for kc in range(8)])
        act(xTb[:, :, :], xTf[:, :, :], AF.Copy, R=["xTf"], W=["xTb"])
        if own:
            dma("pool", mkp[:, :, :], dr["maskp"][n], W=["mkp"])

        build.stage(10)
        b = bp.get()
        for kc in range(8):
            mm(B[b], wga[:, kc, :], xTb[:, kc, :], kc == 0, kc == 7, R=["wga", "xTb"], W=[bk(b)])
        act(gaT[0:16, :], B[b][0:16, :], AF.Copy, R=[bk(b)], W=["gaT"])
        bp.put(b)
        for pair in range(2):
            b = bp.get()
            mm(B[b], wa2[:, pair * 128:(pair + 1) * 128], gaT[:, :], R=["wa2", "gaT"], W=[bk(b)])
            act(sc[pair], B[b], AF.Exp, scale=-1.0, R=[bk(b)], W=[("sc", pair)])
            bp.put(b)
            act(sc[pair], sc[pair], AF.Ln, bias=1.0, R=[("sc", pair)], W=[("sc", pair)])
            S.add("dve", (lambda o, d1: (lambda e: e.tensor_tensor_scan(out=o, data0=reset, data1=d1, initial=0.0,
                                                                          op0=ALU.mult, op1=ALU.add)))(sc[2 + pair][:, :], sc[pair][:, :]),
                  R=[("sc", pair), "cst"], W=[("sc", 2 + pair)])
            act(sc[4 + pair], sc[2 + pair], AF.Exp, scale=-1.0 / 16, R=[("sc", 2 + pair)], W=[("sc", 4 + pair)])
            act(sc[pair], sc[2 + pair], AF.Exp, scale=1.0 / 16, R=[("sc", 2 + pair)], W=[("sc", pair)])
        if own:
            s, wap = wl_win(CQ_G)
            for pair in range(2):
                b = bp.get()
                proj_fm(b, s, wap, pair * 128)
                for hl in range(2):
                    hb = hl * 64
                    stt(qm[hb:hb + 64, pair * 2 + hl, :], B[b][hb:hb + 64, :], 0.125, sc[4 + pair][hb:hb + 64, :], ALU.mult, ALU.mult,
                        R=[bk(b), ("sc", 4 + pair)], W=["qm"])
                bp.put(b)
        s, wap = wl_win(CK_G)
        for pair in range(2):
            b = bp.get()
            proj_fm(b, s, wap, pair * 128)
            tt(kb[:, pair, :], B[b], sc[pair], ALU.mult, R=[bk(b), ("sc", pair)], W=[("kb", pair)])
            bp.put(b)

        def vproj(c0, dst_fn, tok_fn, key_fn, nblk=4):
            s0, w0 = wl_win(c0)
            s1, w1 = wl_win(c0 + 256)
            for j in range(nblk):
                b = bp.get()
                proj_tm(b, j, s0, w0, lambda kc, j=j: tok_fn(kc, j), 256, 0)
                proj_tm(b, j, s1, w1, lambda kc, j=j: tok_fn(kc, j), 256, 256)
                act(dst_fn(j), B[b], AF.Copy, R=[bk(b)], W=[key_fn(j)])
                bp.put(b)

        dense = lambda kc, j: xTb[:, kc, j * 128:(j + 1) * 128]
        vproj(CV_G, lambda j: vt[:, j, :], dense, lambda j: ("vt", j))

        def chunk_loop(Sst, Sstb, sname, mask_ap, qinter, qikey, koff_t, dec_fn, gla):
            ob = [bp.get() for _ in range(4)] if own else None
            for c in range(4):
                cs = slice(c * 128, (c + 1) * 128)
                ci = c % 2
                if own:
                    ab = bp.get()
                    for h in range(4):
                        pair, hl = divmod(h, 2)
                        hb = hl * 64
                        mm(B[ab][:, h * 128:(h + 1) * 128], kb[:, pair, cs], qm[:, h, cs],
                           R=[("kb", pair), "qm"], W=[bk(ab)])
                    tt(att[:, ci, :].rearrange("p (a b) -> p a b", a=4) if gla else att[:, ci, :],
                       B[ab][:, :].rearrange("p (a b) -> p a b", a=4) if gla else B[ab][:, :],
                       mask_ap, ALU.mult, R=[bk(ab), "cst"], W=[("att", ci)])
                    bp.put(ab)
                for pair in range(2):
                    tr(BT[:, pair * 128:(pair + 1) * 128], kb[:, koff_t + pair, cs], ident_b,
                       R=[("kb", koff_t + pair), "cb"], W=["BT"])
                act(ktk[:, ci, :], BT[:, 0:256], AF.Copy, R=["BT"], W=[("ktk", ci)])
                if own:
                    for h in range(4):
                        pair, hl = divmod(h, 2)
                        hb = hl * 64
                        mm(B[ob[h]][:, cs], vt[:, c, h * 128:(h + 1) * 128], att[:, ci, h * 128:(h + 1) * 128],
                           c == 0, False, R=[("vt", c), ("att", ci)], W=[bk(ob[h])])
                        mm(B[ob[h]][:, cs], Sstb[:, pair, :], qinter[:, h, cs],
                           False, True, R=[sname + "b", qikey], W=[bk(ob[h])])
                kvb = bp.get()
                for pair in range(2):
                    mm(B[kvb][:, pair * 256:(pair + 1) * 256], ktk[:, ci, pair * 128:(pair + 1) * 128],
                       vt[:, c, pair * 256:(pair + 1) * 256], R=[("ktk", ci), ("vt", c)], W=[bk(kvb)])
                for pair in range(2):
                    for hl in range(2):
                        hb = hl * 64
                        rows = slice(hb, hb + 64)
                        kvs = B[kvb][rows, pair * 256 + hl * 128:pair * 256 + (hl + 1) * 128]
                        dec = dec_fn(pair, c)[rows, :]
                        if gla:
                            ts(stmp[rows, pair, :], kvs, dec, None, ALU.mult, None,
                               R=[bk(kvb), ("sc", 4 + pair)], W=["stmp"])
                            stt(Sst[rows, pair, :], Sst[rows, pair, :], dec, stmp[rows, pair, :], ALU.mult, ALU.add,
                                R=[sname, "stmp", ("sc", 4 + pair)], W=[sname])
                        else:
                            stt(Sst[rows, pair, :], Sst[rows, pair, :], dec, kvs, ALU.mult, ALU.add,
                                R=[sname, bk(kvb), "cst"], W=[sname])
                bp.put(kvb)
                act(Sstb[:, :, :], Sst[:, :, :], AF.Copy, R=[sname], W=[sname + "b"])
            return ob

        ob = chunk_loop(Sg, Sgb, "Sg", causal4, qm, "qm", 0,
                        lambda pair, c: sc[4 + pair][:, c * 128 + 127:c * 128 + 128], True)
        if own:
            s0, w0 = wl_win(CR_G)
            s1, w1 = wl_win(CR_G + 256)
            for h in range(4):
                b = bp.get()
                proj_fm(b, (s0, s1)[h // 2], (w0, w1)[h // 2], (h % 2) * 128)
                act(sc[0], B[b], AF.Silu, R=[bk(b)], W=[("sc", 0)])
                bp.put(b)
                act(ee[:, 0, :], B[ob[h]], AF.Square, R=[bk(ob[h])], W=[("ee", 0)])
                b = bp.get()
                mm(B[b], ones_b, ee[:, 0, :], R=[("ee", 0), "cb"], W=[bk(b)])
                act(sc[1], B[b], AF.Sqrt, scale=1.0 / 128, bias=1e-6, R=[bk(b)], W=[("sc", 1)])
                bp.put(b)
                recip(sc[1], sc[1], R=[("sc", 1)], W=[("sc", 1)])
                stt(sc[2], B[ob[h]], prm[:, P_GG + h:P_GG + h + 1], sc[1], ALU.mult, ALU.mult,
                    R=[bk(ob[h]), "prm", ("sc", 1)], W=[("sc", 2)])
                tt(oT["oa"][:, h, :], sc[2], sc[0], ALU.mult, R=[("sc", 2), ("sc", 0)], W=[("oa", h)])
                bp.put(ob[h])

        build.stage(20)
        dma("sp", rot[:, :, :], dr["rot"][:, 0:2, t0:t0 + TT], W=["rot"])
        GQ = cst[:, C_GQ:C_GQ + 256].rearrange("p (a b) -> p a b", a=2)
        GK = cst[:, C_GK:C_GK + 256].rearrange("p (a b) -> p a b", a=2)
        for isk in ((0, 1) if own else (1,)):
            s, wap = wl_win(CK_R if isk else CQ_R)
            for pair in range(2):
                b = bp.get()
                proj_fm(b, s, wap, pair * 128)
                fin = rotary(b, None, permr_b)
                bp.put(b)
                if isk:
                    fin(kb[:, pair, :], [("kb", pair)])
                    tt(kb[:, 2 + pair, :].rearrange("p (a b) -> p a b", a=4), kb[:, pair, :].rearrange("p (a b) -> p a b", a=4),
                       GK[:, pair, :].unsqueeze(1).broadcast_to([128, 4, 128]), ALU.mult,
                       R=[("kb", pair), "cst"], W=[("kb", 2 + pair)])
                else:
                    fin(qb[:, pair, :], [("qb", pair)])
                    for hl in range(2):
                        hb = hl * 64
                        rows = slice(hb, hb + 64)
                        vcopy(qm[rows, pair * 2 + hl, :], qb[rows, pair, :], R=[("qb", pair)], W=["qm"])
                        tt(qd[rows, pair * 2 + hl, :].rearrange("p (a b) -> p a b", a=4), qb[rows, pair, :].rearrange("p (a b) -> p a b", a=4),
                           GQ[rows, pair, :].unsqueeze(1).broadcast_to([64, 4, 128]), ALU.mult,
                           R=[("qb", pair), "cst"], W=["qd"])
        vproj(CV_R, lambda j: vt[:, j, :], dense, lambda j: ("vt", j))
        ob = chunk_loop(Sr, Srb, "Sr", dmaskr, qd, "qd", 2,
                        lambda pair, c: cst[:, C_CD + pair:C_CD + pair + 1], False)
        if own:
            s0, w0 = wl_win(CR_R)
            s1, w1 = wl_win(CR_R + 256)
            for h in range(4):
                b = bp.get()
                proj_fm(b, (s0, s1)[h // 2], (w0, w1)[h // 2], (h % 2) * 128)
                act(sc[0], B[b], AF.Silu, R=[bk(b)], W=[("sc", 0)])
                bp.put(b)
                act(sc[1], B[ob[h]], AF.Copy, R=[bk(ob[h])], W=[("sc", 1)])
                act(ee[:, 0, :], B[ob[h]], AF.Copy, R=[bk(ob[h])], W=[("ee", 0)])
                bp.put(ob[h])
                b = bp.get()
                mm(B[b], od128_b, ee[:, 0, :], R=[("ee", 0), "cb"], W=[bk(b)])
                tt(sc[1], sc[1], B[b], ALU.subtract, R=[("sc", 1), bk(b)], W=[("sc", 1)])
                bp.put(b)
                act(ee[:, 1, :], sc[1], AF.Square, R=[("sc", 1)], W=[("ee", 1)])
                b = bp.get()
                mm(B[b], od128_b, ee[:, 1, :], R=[("ee", 1), "cb"], W=[bk(b)])
                act(sc[2], B[b], AF.Sqrt, bias=1e-5, R=[bk(b)], W=[("sc", 2)])
                bp.put(b)
                recip(sc[2], sc[2], R=[("sc", 2)], W=[("sc", 2)])
                stt(sc[3], sc[1], prm[:, P_RG + h:P_RG + h + 1], sc[2], ALU.mult, ALU.mult,
                    R=[("sc", 1), "prm", ("sc", 2)], W=[("sc", 3)])
                tt(oT["ob"][:, h, :], sc[3], sc[0], ALU.mult, R=[("sc", 3), ("sc", 0)], W=[("ob", h)])

        build.stage(30)
        if need3 or need12:
            dma("sp", rot[:, :, :], dr["rot"][:, 2:4, t0:t0 + TT], W=["rot"])
        if need12:
            vproj(CV_D, lambda j: V1[:, j, :], dense, lambda j: "V1")
            build.stage(31)
            vproj(CV_D + 512, lambda j: V2[par][:, j, :], lambda kc, j: xTb[:, kc, j:TT:4], lambda j: ("V2", par))
        build.stage(32)
        if need3:
            vproj(CV_D + 1024, lambda j: vt[:, j, :], dense, lambda j: ("vt", j))
            dma("sp", dr["v3scr"][par].rearrange("(s p) e -> p s e", p=128), vt[:, :, :],
                R=[("vt", j) for j in range(4)], W=[("v3scr", par)])
            if own:
                vproj(CV_D + 1024, lambda j: V3x[:, j, :], lambda kc, j: xTb[:, kc, j:TT:4], lambda j: "V3x")
        build.stage(33)
        Kcur = [K1, K2[par], K3c]
        Kkey = ["K1", ("K2", par), "K3c"]
        scale = 128.0 ** -0.5
        for h in range(4):
            for g in range(3):
                if (g < 2 and not need12) or (g == 2 and not need3):
                    continue
                s, wap = wl_win(CQK_D + (h * 6 + g * 2) * 128)
                if own:
                    b = bp.get()
                    proj_fm(b, s, wap, 0)
                    fin = rotary(b, None, permd_b)
                    bp.put(b)
                    fin(qb[:, g, :], [("qb", g)])
                b = bp.get()
                proj_fm(b, s, wap, 128)
                fin = rotary(b, None, permd_b)
                bp.put(b)
                fin(Kcur[g][:, h, :], [Kkey[g]])
            if not own:
                continue
            nb, db_ = bp.get(), bp.get()
            first = [True]
            pi = [0]

            def pv(lhsT_v, pslice, outcols, Rv, rows=128, lastone=False):
                mm(B[nb][:, outcols], lhsT_v, pslice, first[0], lastone, R=Rv, W=[bk(nb)])
                mm(B[db_][:, outcols], ones_b[0:rows, :], pslice, first[0], lastone, R=Rv + ["cb"], W=[bk(db_)])
                first[0] = False

            def softpart(sbank, mask_ap, Rm, rows=128, view=None):
                ei = pi[0] % 2
                pj = pi[0] % 2
                pi[0] += 1
                act(ee[0:rows, ei, :], B[sbank][0:rows, :], AF.Exp, scale=scale, R=[bk(sbank)], W=[("ee", ei)])
                o = pp[0:rows, pj, :]
                i0 = ee[0:rows, ei, :]
                if view is not None:
                    o, i0 = view(o), view(i0)
                tt(o, i0, mask_ap, ALU.mult, R=[("ee", ei)] + Rm, W=[("pp", pj)])
                return pj

            hs = slice(h * 128, (h + 1) * 128)
            v4 = lambda a: a.rearrange("p (a b) -> p a b", a=4)
            v16 = lambda a: a.rearrange("p (a b) -> p a b", a=16)
            sbk = bp.get()
            for bi in range(4):
                lk = K1p[:, h, :] if bi == 0 else K1[:, h, (bi - 1) * 128:bi * 128]
                mm(B[sbk][:, bi * 128:(bi + 1) * 128], lk, qb[:, 0, bi * 128:(bi + 1) * 128],
                   R=["K1p", "K1", ("qb", 0)], W=[bk(sbk)])
            pj = softpart(sbk, mkp[:, 0, :], ["mkp"])
            bp.put(sbk)
            for bi in range(4):
                lv = V1p[:, hs] if bi == 0 else V1[:, bi - 1, hs]
                pv(lv, pp[:, pj, bi * 128:(bi + 1) * 128], slice(bi * 128, (bi + 1) * 128), ["V1p", "V1", ("pp", pj)])
            sbk = bp.get()
            for bi in range(4):
                mm(B[sbk][:, bi * 128:(bi + 1) * 128], K1[:, h, bi * 128:(bi + 1) * 128], qb[:, 0, bi * 128:(bi + 1) * 128],
                   R=["K1", ("qb", 0)], W=[bk(sbk)])
            pj = softpart(sbk, causal4, ["cst"], view=v4)
            bp.put(sbk)
            for bi in range(4):
                pv(V1[:, bi, hs], pp[:, pj, bi * 128:(bi + 1) * 128], slice(bi * 128, (bi + 1) * 128), ["V1", ("pp", pj)])
            for prev in (1, 0):
                Kx, Vx = (K2[1 - par], V2[1 - par]) if prev else (K2[par], V2[par])
                kx, vx = (("K2", 1 - par), ("V2", 1 - par)) if prev else (("K2", par), ("V2", par))
                sbk = bp.get()
                for c in range(4):
                    mm(B[sbk][:, c * 128:(c + 1) * 128], Kx[:, h, c:TT:4], qb[:, 1, c:TT:4], R=[kx, ("qb", 1)], W=[bk(sbk)])
                if prev:
                    pj = softpart(sbk, mkp[:, 1, :], ["mkp"])
                else:
                    pj = softpart(sbk, causal4, ["cst"], view=v4)
                bp.put(sbk)
                for c in range(4):
                    pv(Vx[:, c, hs], pp[:, pj, c * 128:(c + 1) * 128], slice(c, TT, 4), [vx, ("pp", pj)])
            sbk = bp.get()
            for c in range(16):
                mm(B[sbk][:, c * 32:(c + 1) * 32], K3r[:, h, c, :], qb[:, 2, c:TT:16], R=["K3r", ("qb", 2)], W=[bk(sbk)])
            pj = softpart(sbk, mkp[:, 2, :], ["mkp"])
            bp.put(sbk)
            for c in range(16):
                pv(V3r[:, c, hs], pp[:, pj, c * 32:(c + 1) * 32], slice(c, TT, 16), ["V3r", ("pp", pj)])
            sbk = bp.get()
            for c in range(4):
                mm(B[sbk][:, c * 128:(c + 1) * 128], K3c[:, h, c:TT:4], qb[:, 2, c:TT:4], R=["K3c", ("qb", 2)], W=[bk(sbk)])
            pj = softpart(sbk, m3c4, ["cst"], view=v4)
            bp.put(sbk)
            for c in range(4):
                pv(V3x[:, c, hs], pp[:, pj, c * 128:(c + 1) * 128], slice(c, TT, 4), ["V3x", ("pp", pj)], lastone=(c == 3))
            recip(sc[0], B[db_], R=[bk(db_)], W=[("sc", 0)])
            tt(oT["oc"][:, h, :], B[nb], sc[0], ALU.mult, R=[bk(nb), ("sc", 0)], W=[("oc", h)])
            bp.put(nb)
            bp.put(db_)
        build.stage(34)
        if need3:
            pb = (32 * T) % 128
            for h in range(4):
                vcopy(K3r[:, h, :, pb:pb + 32], K3c[:, h, :].rearrange("p (j c) -> p c j", c=16), R=["K3c"], W=["K3r"])
            build.stage(35)
            dma("sp", V3r[pb:pb + 32, :, :], dr["v3scr"][par].rearrange("(j c) e -> j c e", c=16),
                R=[("v3scr", par)], W=["V3r"])
        build.stage(36)
        if need12:
            vcopy(K1p[:, :, :], K1[:, :, 384:512], R=["K1"], W=["K1p"])
            vcopy(V1p[:, :], V1[:, 3, :], R=["V1"], W=["V1p"])
        if not own:
            return
        if debug and last:
            for nm in ("oa", "ob", "oc"):
                dma("pool", dbg[nm], oT[nm][:, :, :], R=[(nm, h) for h in range(4)], W=[("dbg", nm)])

        build.stage(40)
        for cc in range(8):
            for br in range(3):
                ci = cc * 3 + br
                s, wap = wload(wgbv[:, :, ci * 128:(ci + 1) * 128], 12, 128, "wgb")
                bg = bp.get()
                proj_fm(bg, s, wap, 0)
                act(sc[br], B[bg], AF.Sigmoid, bias=prm[:, P_BG + ci:P_BG + ci + 1], R=[bk(bg), "prm"], W=[("sc", br)])
                bp.put(bg)
                by = bp.get()
                on = ("oa", "ob", "oc")[br]
                for kc in range(4):
                    mm(B[by], wap[:, 8 + kc, 0:128], oT[on][:, kc, :], kc == 0, kc == 3, R=[("w", s), (on, kc)], W=[bk(by)])
                if br == 0:
                    tt(sc[4], B[by], sc[br], ALU.mult, R=[bk(by), ("sc", br)], W=[("sc", 4)])
                else:
                    tt(sc[5], B[by], sc[br], ALU.mult, R=[bk(by), ("sc", br)], W=[("sc", 5)])
                    if br == 1:
                        tt(sc[4], sc[4], sc[5], ALU.add, R=[("sc", 4), ("sc", 5)], W=[("sc", 4)])
                    else:
                        tt(mgT[:, cc, :], sc[4], sc[5], ALU.add, R=[("sc", 4), ("sc", 5)], W=["big"])
                bp.put(by)
        if debug and last:
            dma("pool", dbg["mg"], mgT, R=["big"], W=[("dbg", "mg")])

        def layer_norm(gcol, bcol):
            mb = bp.get()
            for kc in range(8):
                mm(B[mb], od1024_b, xTb[:, kc, :], kc == 0, kc == 7, R=["xTb", "cb"], W=[bk(mb)])
            for kc in range(8):
                tt(xTf[:, kc, :], xTf[:, kc, :], B[mb], ALU.subtract, R=[("xTf", kc), bk(mb)], W=[("xTf", kc)])
                act(xTb[:, kc, :], xTf[:, kc, :], AF.Square, R=[("xTf", kc)], W=["xTb"])
            bp.put(mb)
            vb = bp.get()
            for kc in range(8):
                mm(B[vb], od1024_b, xTb[:, kc, :], kc == 0, kc == 7, R=["xTb", "cb"], W=[bk(vb)])
            act(sc[0], B[vb], AF.Sqrt, bias=1e-5, R=[bk(vb)], W=[("sc", 0)])
            bp.put(vb)
            recip(sc[0], sc[0], R=[("sc", 0)], W=[("sc", 0)])
            for kc in range(8):
                tt(xTf[:, kc, :], xTf[:, kc, :], sc[0], ALU.mult, R=[("xTf", kc), ("sc", 0)], W=[("xTf", kc)])
                act(xTf[:, kc, :], xTf[:, kc, :], AF.Identity, scale=prm[:, gcol + kc:gcol + kc + 1],
                    bias=prm[:, bcol + kc:bcol + kc + 1], R=[("xTf", kc), "prm"], W=[("xTf", kc)])
                act(xTb[:, kc, :], xTf[:, kc, :], AF.Copy, R=[("xTf", kc)], W=["xTb"])

        build.stage(50)
        for j in range(4):
            s, wap = wload(woutv[:, :, j * 256:(j + 1) * 256], 8, 256, "wout")
            for i in range(2):
                cc = 2 * j + i
                b = bp.get()
                for kc in range(8):
                    mm(B[b], wap[:, kc, i * 128:(i + 1) * 128], mgT[:, kc, :], kc == 0, kc == 7, R=[("w", s), "big"], W=[bk(b)])
                stt(xTf[:, cc, :], xTf[:, cc, :], ALPHA, B[b], ALU.mult, ALU.add, R=["xTf", ("xTf", cc), bk(b)], W=[("xTf", cc)])
                bp.put(b)
                act(xTb[:, cc, :], xTf[:, cc, :], AF.Copy, R=[("xTf", cc)], W=["xTb"])
        layer_norm(P_L1G, P_L1B)
        if debug and last:
            dma("sp", dbg["x1"], xTf[:, :, :], R=[("xTf", kc) for kc in range(8)], W=[("dbg", "x1")])

        build.stage(60)
        for blk in range(11):
            sg, wg_ = wload(wgv[:, :, blk * 256:(blk + 1) * 256], 8, 256, "wg")
            su, wu_ = wload(wuv[:, :, blk * 256:(blk + 1) * 256], 8, 256, "wu")
            for i in range(2):
                fc = 2 * blk + i
                b1 = bp.get()
                proj_fm(b1, sg, wg_, i * 128)
                act(sc[i], B[b1], AF.Silu, R=[bk(b1)], W=[("sc", i)])
                bp.put(b1)
                b2 = bp.get()
                proj_fm(b2, su, wu_, i * 128)
                tt(actT[:, fc, :], B[b2], sc[i], ALU.mult, R=[bk(b2), ("sc", i)], W=["big"])
                bp.put(b2)
        for j in range(4):
            slots = []
            for g3 in range(3):
                nk = 8 if g3 < 2 else 6
                s, wap = wload(wdv[:, g3 * 8:g3 * 8 + nk, j * 256:(j + 1) * 256], nk, 256, "wd")
                slots.append((s, wap))
            for i in range(2):
                cc = 2 * j + i
                b = bp.get()
                for kc in range(22):
                    s, wap = slots[kc // 8]
                    mm(B[b], wap[:, kc % 8, i * 128:(i + 1) * 128], actT[:, kc, :], kc == 0, kc == 21, R=[("w", s), "big"], W=[bk(b)])
                stt(xTf[:, cc, :], xTf[:, cc, :], ALPHA, B[b], ALU.mult, ALU.add, R=[("xTf", cc), bk(b)], W=[("xTf", cc)])
                bp.put(b)
                act(xTb[:, cc, :], xTf[:, cc, :], AF.Copy, R=[("xTf", cc)], W=["xTb"])
        layer_norm(P_L2G, P_L2B)
        dma("sp", yTv[:, :, n * TT:(n + 1) * TT], xTf[:, :, :], R=[("xTf", kc) for kc in range(8)] + ["xTf"], W=[("y", n)])

    import os as _os
    _stop = int(_os.environ.get("MK_STOP", "1000000000"))

    class _Stop(Exception):
        pass

    def stage(k):
        if build.curT * 100 + k > _stop:
            raise _Stop()
    build.stage = stage
    try:
        for T in range(NT):
            build.curT = T
            tile_step(T)
    except _Stop:
        pass
    S.emit(nc, es, None)
    es.close()
    return nc, S


IN_W = (256, 256, 512, 512, 16, 256, 256, 512, 512, 1536, 1536, 1536, 3072)
IN_OFF = np.concatenate([[0], np.cumsum(IN_W)])


def _consts():
    c = np.zeros((128, NCST), np.float32)
    c[:, C_ID:C_ID + 128] = np.eye(128, dtype=np.float32)
    pr = np.zeros((128, 128), np.float32)
    for m in range(128):
        hd, dd = divmod(m, 64)
        pr[hd * 64 + (dd + 32) % 64, m] = 1.0
    c[:, C_PR:C_PR + 128] = pr
    pd = np.zeros((128, 128), np.float32)
    for m in range(32):
        pd[(m + 16) % 32, m] = 1.0
    c[:, C_PD:C_PD + 128] = pd
    j = np.arange(128)[:, None]
    i = np.arange(128)[None, :]
    c[:, C_CA:C_CA + 128] = (j <= i).astype(np.float32)
    log_g = np.log1p(-np.exp2(-5.0 - np.arange(4, dtype=np.float32))).astype(np.float32)
    for h in range(4):
        dm = np.where(i >= j, np.exp(np.maximum(i - j, 0).astype(np.float32) * log_g[h]), 0.0) / 8.0
        c[:, C_DM + h * 128:C_DM + (h + 1) * 128] = dm.astype(np.float32)
    idx = np.arange(128, dtype=np.float32)
    for pair in range(2):
        for hl in range(2):
            h = pair * 2 + hl
            c[hl * 64:(hl + 1) * 64, C_GQ + pair * 128:C_GQ + (pair + 1) * 128] = np.exp((idx + 1.0) * log_g[h])[None, :]
            c[hl * 64:(hl + 1) * 64, C_GK + pair * 128:C_GK + (pair + 1) * 128] = (np.exp((127.0 - idx) * log_g[h]) / 8.0)[None, :]
            c[hl * 64:(hl + 1) * 64, C_CD + pair] = np.exp(128.0 * log_g[h])
    c[:, C_M3:C_M3 + 128] = ((j <= i) & ((i - j) % 4 == 0)).astype(np.float32)
    rs = np.ones(512, np.float32)
    rs[0::128] = 0.0
    c[:, C_RS:C_RS + 512] = rs[None, :]
    return c


def _rot_table(pos):
    L = pos.shape[0]
    out = np.zeros((128, 4, L), np.float32)
    inv = (np.float32(10000.0) ** (-np.arange(32, dtype=np.float32) * np.float32(2.0) / np.float32(64))).astype(np.float32)
    ang = (pos[:, None] * inv[None, :]).astype(np.float32)
    co, si = np.cos(ang).T.astype(np.float32), np.sin(ang).T.astype(np.float32)
    for hl in range(2):
        out[hl * 64:hl * 64 + 32, 0] = co
        out[hl * 64 + 32:hl * 64 + 64, 0] = co
        out[hl * 64:hl * 64 + 32, 1] = -si
        out[hl * 64 + 32:hl * 64 + 64, 1] = si
    inv = (np.float32(500000.0) ** (-np.arange(16, dtype=np.float32) * np.float32(2.0) / np.float32(32))).astype(np.float32)
    ang = (pos[:, None] * inv[None, :]).astype(np.float32)
    co, si = np.cos(ang).T.astype(np.float32), np.sin(ang).T.astype(np.float32)
    out[:, 2] = 1.0
    out[0:16, 2] = co
    out[16:32, 2] = co
    out[0:16, 3] = -si
    out[16:32, 3] = si
    return out


def _maskp(NT_CTX, NT_OWN, gstart):
    m = np.zeros((NT_OWN, 128, 3, 512), np.float32)
    jj = np.arange(128)[:, None]
    for n in range(NT_OWN):
        T = NT_CTX + n
        for bi in range(4):
            ii = np.arange(128)[None, :]
            ktok = T * 512 + (bi - 1) * 128 + jj
            ok = (jj >= ii) & (ktok + gstart >= 0)
            m[n, :, 0, bi * 128:(bi + 1) * 128] = ok
        for c in range(4):
            ii = np.arange(128)[None, :]
            ktok = 4 * (128 * (T - 1) + jj) + c
            ok = (jj >= ii) & (ktok + gstart >= 0)
            m[n, :, 1, c * 128:(c + 1) * 128] = ok
        for c in range(16):
            ii = np.arange(32)[None, :]
            r = (jj - 32 * T) % 128
            mk = 32 * T - 128 + r
            ktok = 16 * mk + c
            ok = (r >= ii) & (ktok + gstart >= 0)
            m[n, :, 2, c * 32:(c + 1) * 32] = ok
    return m


def _prep_layer(w_in, w_gla_a2, b_gla_a, gla_norm_g, ret_norm_g, w_branch, b_gate, w_out,
                ln1_g, ln1_b, w_ffn_gate, w_ffn_up, w_ffn_down, ln2_g, ln2_b):
    o = IN_OFF
    sec = {nm: w_in[:, o[i]:o[i + 1]] for i, nm in enumerate(
        ("gq", "gk", "gv", "gr", "ga", "rq", "rk", "rv", "rg", "dq", "dk", "dv", "gate"))}
    cols = [sec["gq"], sec["gk"], sec["gv"], sec["gr"], sec["rq"], sec["rk"], sec["rv"], sec["rg"]]
    for g in range(3):
        cols.append(sec["dv"][:, g * 512:(g + 1) * 512])
    for h in range(4):
        for g in range(3):
            cols.append(sec["dq"][:, g * 512 + h * 128:g * 512 + (h + 1) * 128])
            cols.append(sec["dk"][:, g * 512 + h * 128:g * 512 + (h + 1) * 128])
    win = np.ascontiguousarray(np.concatenate(cols, axis=1))
    assert win.shape == (D, 7680)
    wa2 = np.zeros((128, 256), np.float32)
    wa2[0:16] = w_gla_a2
    wa2[16] = b_gla_a
    wgb = np.zeros((1536, 3072), np.float32)
    for cc in range(8):
        for br in range(3):
            ci = cc * 3 + br
            wgb[0:1024, ci * 128:(ci + 1) * 128] = sec["gate"][:, br * 1024 + cc * 128:br * 1024 + (cc + 1) * 128]
            wgb[1024:1536, ci * 128:(ci + 1) * 128] = w_branch[br][:, cc * 128:(cc + 1) * 128]
    prm = np.zeros((128, 64), np.float32)
    for cc in range(8):
        for br in range(3):
            prm[:, P_BG + cc * 3 + br] = b_gate[br * 1024 + cc * 128:br * 1024 + (cc + 1) * 128]
    prm[:, P_GG:P_GG + 4] = gla_norm_g.reshape(4, 128).T
    prm[:, P_RG:P_RG + 4] = ret_norm_g.reshape(4, 128).T
    prm[:, P_L1G:P_L1G + 8] = ln1_g.reshape(8, 128).T
    prm[:, P_L1B:P_L1B + 8] = ln1_b.reshape(8, 128).T
    prm[:, P_L2G:P_L2G + 8] = ln2_g.reshape(8, 128).T
    prm[:, P_L2B:P_L2B + 8] = ln2_b.reshape(8, 128).T
    return {"win": win, "wga": np.ascontiguousarray(np.concatenate([sec["ga"], np.zeros((D, 112), np.float32)], axis=1)), "wa2": wa2, "wgb": wgb,
            "wout": np.ascontiguousarray(w_out), "wg": np.ascontiguousarray(w_ffn_gate),
            "wu": np.ascontiguousarray(w_ffn_up), "wd": np.ascontiguousarray(w_ffn_down), "prm": prm}


_CACHE = {}


def _get_prog(NT_CTX, NT_OWN, debug=False):
    key = (NT_CTX, NT_OWN, debug)
    if key not in _CACHE:
        _CACHE[key] = build(NT_CTX, NT_OWN, debug)
    return _CACHE[key][0]


def run_layer(xT_full, lw, NT_HALF, debug=False):
    Bn, _, Sq = xT_full.shape
    half = NT_HALF * TT
    assert Sq == 2 * half
    nc = _get_prog(NT_HALF, NT_HALF, debug)
    cst = _consts()
    in_maps = []
    for b in range(Bn):
        for hf in range(2):
            if hf == 0:
                xin = np.concatenate([np.zeros((D, half), np.float32), xT_full[b][:, :half]], axis=1)
                gstart = -half
            else:
                xin = xT_full[b]
                gstart = 0
            pos = (np.arange(2 * half, dtype=np.float32) + np.float32(gstart)).astype(np.float32)
            m = {"xT": np.ascontiguousarray(xin), "rot": _rot_table(pos), "maskp": _maskp(NT_HALF, NT_HALF, gstart), "cst": cst}
            m.update(lw)
            in_maps.append(m)
    res = run_bass_kernel_spmd(nc, in_maps, core_ids=list(range(len(in_maps))))
    out = np.zeros_like(xT_full)
    for b in range(Bn):
        for hf in range(2):
            out[b][:, hf * half:(hf + 1) * half] = res.results[b * 2 + hf]["yT"]
    return out, res


def kernel(x, w_in, w_gla_a2, b_gla_a, gla_norm_g, ret_norm_g, w_branch, b_gate, w_out,
           ln1_g, ln1_b, w_ffn_gate, w_ffn_up, w_ffn_down, ln2_g, ln2_b):
    x = np.asarray(x, np.float32)
    Bn, Sq, _ = x.shape
    xT = np.ascontiguousarray(np.transpose(x, (0, 2, 1)))
    args = [np.asarray(a, np.float32) for a in (w_in, w_gla_a2, b_gla_a, gla_norm_g, ret_norm_g, w_branch, b_gate, w_out,
                                                 ln1_g, ln1_b, w_ffn_gate, w_ffn_up, w_ffn_down, ln2_g, ln2_b)]
    for l in range(DEPTH):
        lw = _prep_layer(*[a[l] for a in args])
        xT, _ = run_layer(xT, lw, Sq // (2 * TT))
    return np.ascontiguousarray(np.transpose(xT, (0, 2, 1))).astype(np.float32)
```
